# Optimizing a Trainium2 kernel written in Bass

```python
import jax, jax.numpy as jnp
from jax import lax
import numpy as np

D_MODEL = 1024
BATCH = 32
SEQ = 2048
DEPTH = 1

N_META = 16
D_MIX = D_MODEL
D_CONV = D_MIX // 2
D_POOL = D_MIX - D_CONV
CONV_HEADS = 8
CONV_WIDTH = 3
POOL_WINDOWS = (2, 4, 8, 16)
N_POOL_GROUPS = len(POOL_WINDOWS)
POOL_GROUP = D_POOL // N_POOL_GROUPS
D_IN_PROJ = 3 * D_CONV + D_POOL
D_FF = ((int(np.ceil(8 * D_MODEL / 3)) + 255) // 256) * 256
RMS_EPS = 1e-6

kernel_name = "hymba_conv_pool_hybrid_block"


def rms_norm(x, g):
    xf = x.astype(jnp.float32)
    y = xf * lax.rsqrt(jnp.mean(xf * xf, axis=-1, keepdims=True) + RMS_EPS)
    return (y * g.astype(jnp.float32)).astype(x.dtype)


def causal_short_conv(u, w):
    k_width = w.shape[0]
    seq_len = u.shape[1]
    up = jnp.pad(u, ((0, 0), (k_width - 1, 0), (0, 0)))
    y = w[0] * up[:, 0:seq_len]
    for k in range(1, k_width):
        y = y + w[k] * up[:, k:k + seq_len]
    return y


def multiscale_pool(u, pool_w, pool_scale):
    bsz, seq_len, _ = u.shape
    ug = u.reshape(bsz, seq_len, N_POOL_GROUPS, POOL_GROUP)
    pos = jnp.arange(seq_len)
    outs = []
    for g, win in enumerate(POOL_WINDOWS):
        xg = ug[:, :, g].astype(jnp.float32)
        cs = jnp.cumsum(xg, axis=1)
        cs_prev = jnp.pad(cs, ((0, 0), (win, 0), (0, 0)))[:, :seq_len]
        cnt = jnp.minimum(pos + 1, win).astype(jnp.float32)[None, :, None]
        outs.append((cs - cs_prev) / cnt - xg)
    pooled = jnp.stack(outs, axis=2).astype(u.dtype)
    mixed = jnp.einsum('blgc,gcd->blgd', pooled, pool_w)
    return mixed.reshape(bsz, seq_len, D_POOL) * pool_scale


def setup_inputs(seed: int = 0) -> dict:
    key = jax.random.key(seed)
    ks = jax.random.split(key, 16)
    f32 = jnp.float32

    def nrm(k, shape, scale):
        return jax.random.normal(k, shape, f32) * scale

    def gain(k):
        return 1.0 + 0.05 * jax.random.normal(k, (DEPTH, D_MODEL), f32)

    return {
        "x": jax.random.normal(ks[0], (BATCH, SEQ, D_MODEL), f32),
        "meta_tokens": nrm(ks[1], (N_META, D_MODEL), 1.0),
        "norm_mix_pre": gain(ks[2]),
        "w_in": nrm(ks[3], (DEPTH, D_MODEL, D_IN_PROJ), D_MODEL ** -0.5),
        "conv_w": nrm(ks[4], (DEPTH, CONV_WIDTH, D_CONV), CONV_WIDTH ** -0.5),
        "pool_w": nrm(ks[5], (DEPTH, N_POOL_GROUPS, POOL_GROUP, POOL_GROUP), POOL_GROUP ** -0.5),
        "pool_scale": 1.0 + 0.1 * jax.random.normal(ks[6], (DEPTH, D_POOL), f32),
        "w_out": nrm(ks[7], (DEPTH, D_MIX, D_MODEL), D_MIX ** -0.5),
        "norm_mix_post": gain(ks[8]),
        "norm_ffn_pre": gain(ks[9]),
        "w_gate": nrm(ks[10], (DEPTH, D_MODEL, D_FF), D_MODEL ** -0.5),
        "w_up": nrm(ks[11], (DEPTH, D_MODEL, D_FF), D_MODEL ** -0.5),
        "w_down": nrm(ks[12], (DEPTH, D_FF, D_MODEL), D_FF ** -0.5),
        "norm_ffn_post": gain(ks[13]),
    }


def reference(x, meta_tokens, norm_mix_pre, w_in, conv_w, pool_w, pool_scale, w_out,
              norm_mix_post, norm_ffn_pre, w_gate, w_up, w_down, norm_ffn_post):
    bsz = x.shape[0]
    meta = jnp.broadcast_to(meta_tokens[None].astype(x.dtype), (bsz, N_META, D_MODEL))
    h = jnp.concatenate([meta, x], axis=1)

    for i in range(DEPTH):
        a = rms_norm(h, norm_mix_pre[i])
        z = a @ w_in[i]
        b_gate = z[..., 0:D_CONV]
        c_gate = z[..., D_CONV:2 * D_CONV]
        v = z[..., 2 * D_CONV:3 * D_CONV]
        p = z[..., 3 * D_CONV:]
        y_conv = b_gate * causal_short_conv(c_gate * v, conv_w[i])
        y_pool = multiscale_pool(p, pool_w[i], pool_scale[i])
        m = jnp.concatenate([y_conv, y_pool], axis=-1) @ w_out[i]
        h = h + rms_norm(m, norm_mix_post[i])

        f = rms_norm(h, norm_ffn_pre[i])
        g = jax.nn.silu(f @ w_gate[i]) * (f @ w_up[i])
        h = h + rms_norm(g @ w_down[i], norm_ffn_post[i])

    return h[:, N_META:]
```

```python
import heapq
from contextlib import ExitStack

import numpy as np
import concourse.bass as bass
import concourse.mybir as mybir
from concourse.bass_utils import run_bass_kernel_spmd

F32 = mybir.dt.float32
BF16 = mybir.dt.bfloat16
AF = mybir.ActivationFunctionType
ALU = mybir.AluOpType

D = 1024
DFF = 2816
NFF = DFF // 128
NMETA = 16
TT = 512
RING = 16
EPS = 1e-6
WINS = (2, 4, 8, 16)
IN_ORDER = [4, 8, 5, 9, 0, 6, 10, 1, 7, 11, 2, 12, 13, 3, 14, 15]

MM = 0.218
TP = 0.068


def t_act(n):
    return 0.22 + n / 1200.0


def t_dve(n, psum=False, fast=False):
    return 0.13 + (0.06 if psum else 0.0) + n / (1640.0 if fast else 960.0)


SYNC_LAT = 0.12


_DBG = {}


class Op:
    __slots__ = ("idx", "eng", "dur", "emit", "deps", "dma", "name", "users", "start", "finish",
                 "sig", "nrem", "delay")

    def __init__(self, idx, eng, dur, emit, deps, dma, name):
        self.idx = idx
        self.eng = eng
        self.dur = dur
        self.emit = emit
        self.deps = deps
        self.dma = dma
        self.name = name
        self.users = []
        self.start = None
        self.finish = None
        self.sig = None
        self.delay = 0.0


class Sched:
    FIFO = ("pe",)

    def __init__(self):
        self.ops = []
        self.last_writer = {}
        self.readers = {}

    def add(self, eng, dur, emit, reads=(), writes=(), dma=None, name="", delay=0.0):
        idx = len(self.ops)
        deps = set()
        for r in reads:
            w = self.last_writer.get(r)
            if w is not None:
                deps.add(w)
        for w_ in writes:
            w = self.last_writer.get(w_)
            if w is not None:
                deps.add(w)
            for rd in self.readers.get(w_, ()):
                deps.add(rd)
        for r in reads:
            self.readers.setdefault(r, []).append(idx)
        for w_ in writes:
            self.last_writer[w_] = idx
            self.readers[w_] = []
        deps.discard(idx)
        op = Op(idx, eng, dur, emit, sorted(deps), dma, name)
        op.delay = 0.0 if _DBG.get("no_delay") else delay
        self.ops.append(op)
        return idx

    def simulate(self):
        ops = self.ops
        for op in ops:
            op.nrem = len(op.deps)
            for d in op.deps:
                ops[d].users.append(op.idx)
        engines = ("pe", "act", "dve", "pool", "sp")
        ready = {e: [] for e in engines}
        busy = {e: False for e in engines}
        order = {e: [] for e in engines}
        fifo_next = {e: 0 for e in engines}
        fifo_list = {e: [op.idx for op in ops if op.eng == e] for e in self.FIFO}
        ev = []
        seq = [0]
        dma_pipe = [0.0]

        def push(t, kind, x):
            seq[0] += 1
            heapq.heappush(ev, (t, seq[0], kind, x))

        def try_start(e, t):
            if busy[e] or not ready[e]:
                return
            if e in self.FIFO:
                want = fifo_list[e][fifo_next[e]]
                if ready[e][0] != want:
                    return
                fifo_next[e] += 1
            i = heapq.heappop(ready[e])
            op = ops[i]
            op.start = t
            order[e].append(i)
            busy[e] = True
            if op.dma is not None:
                push(t + op.dur, "free", e)
                b = op.dma[1]
                dma_pipe[0] = max(dma_pipe[0], t + 0.1) + b / 260e3
                push(dma_pipe[0] + 2.0, "fin", i)
            else:
                push(t + op.dur, "free", e)
                push(t + op.dur, "fin", i)

        for op in ops:
            if op.nrem == 0:
                heapq.heappush(ready[op.eng], op.idx)
        for e in engines:
            try_start(e, 0.0)
        tmax = 0.0
        while ev:
            t, _, kind, x = heapq.heappop(ev)
            tmax = max(tmax, t)
            if kind == "free":
                busy[x] = False
                try_start(x, t)
            elif kind == "rdy":
                heapq.heappush(ready[ops[x].eng], x)
                try_start(ops[x].eng, t)
            else:
                op = ops[x]
                op.finish = t
                for u in op.users:
                    if ops[u].eng != op.eng and ops[u].delay == 0.0:
                        ops[u].delay = -SYNC_LAT
                    uo = ops[u]
                    uo.nrem -= 1
                    if uo.nrem == 0:
                        if uo.delay != 0:
                            push(t + abs(uo.delay), "rdy", u)
                        else:
                            heapq.heappush(ready[uo.eng], u)
                            try_start(uo.eng, t)
        undone = [op.name for op in ops if op.start is None]
        assert not undone, f"scheduler deadlock: {undone[:10]}"
        self.order = order
        self.sim_time = tmax
        return order


def build_program(nseq, nt, verbose=False, debug=False):
    ntile = nseq * nt
    ntok = ntile * TT
    nc = bass.Bass("TRN2", target_bir_lowering=False)
    x_d = nc.dram_tensor("x", [ntok, D], F32, kind="ExternalInput").ap()
    meta_d = nc.dram_tensor("meta_tokens", [NMETA, D], F32, kind="ExternalInput").ap()
    g_pre_d = nc.dram_tensor("norm_mix_pre", [D], F32, kind="ExternalInput").ap()
    w_in_d = nc.dram_tensor("w_in", [D, 2 * D], F32, kind="ExternalInput").ap()
    conv_w_d = nc.dram_tensor("conv_w", [3, 512], F32, kind="ExternalInput").ap()
    pool_w_d = nc.dram_tensor("pool_w", [4, 128, 128], F32, kind="ExternalInput").ap()
    pool_s_d = nc.dram_tensor("pool_scale", [512], F32, kind="ExternalInput").ap()
    w_out_d = nc.dram_tensor("w_out", [D, D], F32, kind="ExternalInput").ap()
    g_post_d = nc.dram_tensor("norm_mix_post", [D], F32, kind="ExternalInput").ap()
    g_ffn_d = nc.dram_tensor("norm_ffn_pre", [D], F32, kind="ExternalInput").ap()
    w_gate_d = nc.dram_tensor("w_gate", [D, DFF], F32, kind="ExternalInput").ap()
    w_up_d = nc.dram_tensor("w_up", [D, DFF], F32, kind="ExternalInput").ap()
    w_down_d = nc.dram_tensor("w_down", [DFF, D], F32, kind="ExternalInput").ap()
    g_fpost_d = nc.dram_tensor("norm_ffn_post", [D], F32, kind="ExternalInput").ap()
    out_d = nc.dram_tensor("out", [ntok, D], F32, kind="ExternalOutput").ap()
    scr = {
        "in": nc.dram_tensor("scr_in", [16, 128, 1024], BF16).ap(),
        "out": nc.dram_tensor("scr_out", [8, 128, 1024], BF16).ap(),
        "gu": nc.dram_tensor("scr_gu", [2 * NFF, 128, 1024], BF16).ap(),
        "dn": nc.dram_tensor("scr_dn", [NFF, 128, 1024], BF16).ap(),
    }

    S = Sched()
    es = ExitStack()

    def sb(name, shape, dt):
        return es.enter_context(nc.sbuf_tensor(name, shape, dt))

    xbuf = [sb(f"xbuf{i}", [128, 4, D], F32) for i in range(3)]
    ring = sb("ring", [128, RING, 1024], BF16)
    NSTAGE = 3
    stage = [sb(f"stage{i}", [128, 1024], F32) for i in range(NSTAGE)]
    abf = [sb(f"abf{i}", [128, D], BF16) for i in range(4)]
    fbf = [sb(f"fbf{i}", [128, D], BF16) for i in range(4)]
    aT = sb("aT", [128, 8, TT], BF16)
    fT = sb("fT", [128, 8, TT], BF16)
    aTm = sb("aTm", [128, 8, NMETA], BF16)
    ubuf = sb("ubuf", [128, 4, TT + 2], F32)
    psb = sb("psb", [128, 4, TT + 16], F32)
    umeta = sb("umeta", [128, 4, NMETA], F32)
    pmeta = sb("pmeta", [128, 4, NMETA], F32)
    t1 = [sb(f"t1_{i}", [128, TT], F32) for i in range(2)]
    t2 = [sb(f"t2_{i}", [128, TT], F32) for i in range(2)]
    wsA = [sb("wsA0", [128, TT + 16], F32)]
    wsB = [sb("wsB0", [128, TT + 16], F32)]
    ybf = sb("ybf", [128, 8, TT], BF16)
    pooled = ybf[:, 4:8, :]
    sil = [sb(f"sil{i}", [128, TT], F32) for i in range(2)]
    gbf = sb("gbf", [128, NFF, TT], BF16)
    tt = [sb(f"tt{i}", [128, D], F32) for i in range(2)]
    gpost_b = sb("gpost_b", [128, D], F32)
    gfpost_b = sb("gfpost_b", [128, D], F32)
    ident = sb("ident", [128, 128], BF16)
    poolw_bf = sb("poolw_bf", [128, 4, 128], BF16)
    gpre_col = sb("gpre_col", [128, 8], F32)
    gffn_col = sb("gffn_col", [128, 8], F32)
    convw_col = sb("convw_col", [128, 3, 4], F32)
    pscale_col = sb("pscale_col", [128, 4], F32)
    nhalf = sb("nhalf", [128, 1], F32)
    st = sb("stats", [128, 64], F32)
    pp = [es.enter_context(nc.psum_tensor(f"pp{i}", [128, 1024], F32)) for i in range(4)]

    def bank(i):
        return pp[i // 2][:, (i % 2) * 512:(i % 2) * 512 + 512]

    def bank_bf(i):
        return bank(i).bitcast(BF16).rearrange("p (k m) -> p k m", k=8)

    def pair(i):
        return pp[i][:, :]

    eng_sem = {e: es.enter_context(nc.semaphore(f"sem_{e}")) for e in ("pe", "act", "dve", "pool")}
    dma_sem = {}
    dma_cnt = {}

    def dsem(key):
        if key not in dma_sem:
            dma_sem[key] = es.enter_context(nc.semaphore("dma_" + "_".join(str(k) for k in key)))
            dma_cnt[key] = 0
        return dma_sem[key]

    def col(base, i):
        return st[:, base + i:base + i + 1]

    C_SSX, C_EX, C_RX = 0, 4, 8
    C_SSM, C_EM, C_RM = 12, 16, 20
    C_SSH, C_EH, C_RH = 24, 28, 32
    C_SSD, C_ED, C_RD = 36, 40, 44

    def rstd_ops(c_ss, c_e, c_r, i, tag):
        S.add("pool", 0.25,
              lambda g, a=col(c_e, i), b=col(c_ss, i): g.tensor_scalar(
                  out=a, in0=b, scalar1=EPS, scalar2=None, op0=ALU.add),
              reads=[("st", c_ss + i)], writes=[("st", c_e + i)], name=f"eps{tag}")
        S.add("pool", 0.6,
              lambda g, a=col(c_r, i), b=col(c_e, i): g.tensor_tensor(
                  out=a, in0=b, in1=nhalf[:], op=ALU.pow),
              reads=[("st", c_e + i), "nhalf"], writes=[("st", c_r + i)], name=f"pow{tag}")

    cloads = []

    def const_setup():
        phase_meta_load()

        S.add("pool", 0.2, lambda g: g.memset(ident[:], 0.0), writes=["ident"], name="ident0")
        S.add("pool", 0.3,
              lambda g: g.affine_select(out=ident[:], in_=ident[:], compare_op=ALU.not_equal, fill=1.0,
                                        base=0, pattern=[[-1, 128]], channel_multiplier=1),
              reads=["ident"], writes=["ident"], name="ident")
        S.add("pool", 0.1, lambda g: g.memset(nhalf[:], -0.5), writes=["nhalf"], name="nhalf")

        def cload(name, out_ap, in_ap, res, strided):
            key = ("c", name)
            dsem(key)

            def f(sp, out_ap=out_ap, in_ap=in_ap, key=key, strided=strided):
                if strided:
                    with nc.allow_non_contiguous_dma(reason="tiny parameter vectors"):
                        return sp.dma_start(out=out_ap, in_=in_ap).then_inc(dma_sem[key], 16)
                return sp.dma_start(out=out_ap, in_=in_ap).then_inc(dma_sem[key], 16)
            S.add("sp", 0.1, f, writes=[res], dma=(key, 64e3), name=f"c_{name}")
        cloads.append(cload)
        cload("gpre", gpre_col[:], g_pre_d.rearrange("(kc p) -> p kc", p=128), "gpre_col", True)

    def const_setup2():
        cload = cloads[0]
        cload("poolw", tt[1][:, 0:512].rearrange("p (g d) -> p g d", g=4), pool_w_d.rearrange("g c d -> c g d"),
              ("tt", 1), False)
        cload("convw", convw_col[:], conv_w_d.rearrange("k (q p) -> p k q", p=128), "convw_col", True)
        cload("pscale", pscale_col[:], pool_s_d.rearrange("(g p) -> p g", p=128), "pscale_col", True)
        cload("gpost", gpost_b[:], g_post_d.rearrange("(o d) -> o d", o=1).partition_broadcast(128), "gpost_b", False)
        cload("gffn", gffn_col[:], g_ffn_d.rearrange("(kc p) -> p kc", p=128), "gffn_col", True)
        cload("gfpost", gfpost_b[:], g_fpost_d.rearrange("(o d) -> o d", o=1).partition_broadcast(128), "gfpost_b", False)
        S.add("act", 0.7,
              lambda a: a.activation(out=poolw_bf[:], in_=tt[1][:, 0:512].rearrange("p (g d) -> p g d", g=4),
                                     func=AF.Copy),
              reads=[("tt", 1)], writes=["poolw_bf"], name="poolw_cast")

    ring_n = [0]
    scr_written = set()
    stage_n = [0]

    def w_src(kind, j):
        if kind == "in":
            return w_in_d[:, j * 128:(j + 1) * 128].rearrange("(kc p) m -> p kc m", p=128), "gpre"
        if kind == "gu":
            w = w_gate_d if j % 2 == 0 else w_up_d
            c = j // 2
            return w[:, c * 128:(c + 1) * 128].rearrange("(kc p) m -> p kc m", p=128), "gffn"
        if kind == "out":
            return w_out_d[j * 128:(j + 1) * 128, :], None
        return w_down_d[j * 128:(j + 1) * 128, :], None

    def ring_load(kind, j):
        n = ring_n[0]
        ring_n[0] += 1
        s = n % RING
        rs = ("ring", s)
        dst = ring[:, s, :]
        skey = (kind, j)
        if skey in scr_written:
            k = ("ring", s)

            def ld(sp, dst=dst, src=scr[kind][j], k=k):
                dsem(k)
                dma_cnt[k] += 16
                return sp.dma_start(out=dst, in_=src).then_inc(dma_sem[k], 16)
            dsem(k)
            S.add("sp", 0.1, ld, reads=[("scr",) + skey], writes=[rs], dma=(k, 256e3), name=f"ld_{kind}{j}")
            return s
        b = stage_n[0] % NSTAGE
        stage_n[0] += 1
        src, fold = w_src(kind, j)
        k1 = ("stage", b)
        dsem(k1)
        if fold is None:
            sdst = stage[b][:, :]
        else:
            sdst = stage[b][:, :].rearrange("p (kc m) -> p kc m", kc=8)

        def ld32(sp, sdst=sdst, src=src, k1=k1):
            dma_cnt[k1] += 16
            return sp.dma_start(out=sdst, in_=src).then_inc(dma_sem[k1], 16)
        if fold is not None and (stage_n[0] % 2 == 0):
            S.add("act", 3.7, ld32, writes=[k1], dma=(k1, 512e3), name=f"ld32_{kind}{j}")
        else:
            S.add("sp", 0.1, ld32, writes=[k1], dma=(k1, 512e3), name=f"ld32_{kind}{j}")
        if fold is None:
            S.add("act", t_act(1024), lambda a, dst=dst, b=b: a.activation(out=dst, in_=stage[b][:, :], func=AF.Copy),
                  reads=[k1], writes=[rs], name=f"cast_{kind}{j}")
        else:
            gcol = gpre_col if fold == "gpre" else gffn_col

            def cast(v, dst=dst, b=b, gcol=gcol):
                return v.tensor_tensor(out=dst.rearrange("p (kc m) -> p kc m", kc=8),
                                       in0=stage[b][:, :].rearrange("p (kc m) -> p kc m", kc=8),
                                       in1=gcol[:, :].unsqueeze(2).broadcast_to([128, 8, 128]),
                                       op=ALU.mult)
            S.add("dve", t_dve(1024), cast, reads=[k1, "gpre_col" if fold == "gpre" else "gffn_col"], writes=[rs], name=f"cast_{kind}{j}")
        k2 = ("scrw", s)
        dsem(k2)

        def park(sp, dst=dst, k2=k2, tgt=scr[kind][j]):
            dma_cnt[k2] += 16
            return sp.dma_start(out=tgt, in_=dst).then_inc(dma_sem[k2], 16)
        S.add("sp", 0.1, park, reads=[rs], writes=[("scr",) + skey], dma=(k2, 256e3), name=f"park_{kind}{j}")
        scr_written.add(skey)
        return s

    def x_rows(k):
        return slice(k * TT, (k + 1) * TT)

    def load_x(k):
        b = k % 3
        key = ("xld", b)
        dsem(key)

        def f(sp, b=b, k=k, key=key):
            dma_cnt[key] += 16
            return sp.dma_start(out=xbuf[b][:, :, :],
                                in_=x_d[x_rows(k), :].rearrange("(g p) d -> p g d", p=128)).then_inc(dma_sem[key], 16)
        S.add("sp", 0.1, f, writes=[("xbuf", b, g) for g in range(4)], dma=(key, 2e6), name=f"ldx{k}",
              delay=(12.0 if k >= 3 else (8.0, 70.0, 140.0)[k]))

    def store_out(k, groups):
        b = k % 3
        key = ("xst", b, groups[0])
        dsem(key)
        g0, g1 = groups[0], groups[-1] + 1

        def f(sp, b=b, k=k, key=key, g0=g0, g1=g1):
            dma_cnt[key] += 16
            r0 = k * TT + g0 * 128
            r1 = k * TT + g1 * 128
            return sp.dma_start(out=out_d[r0:r1, :].rearrange("(g p) d -> p g d", p=128),
                                in_=xbuf[b][:, g0:g1, :]).then_inc(dma_sem[key], 16)
        S.add("sp", 0.1, f, reads=[("xbuf", b, g) for g in groups], writes=[("outdone", k, groups[0])],
              dma=(key, 512e3 * len(groups)), name=f"st{k}_{groups[0]}",
              delay=(0.0 if k == ntile - 1 else 12.0))

    def phase_X(k, tg):
        b = k % 3
        xg = xbuf[b][:, tg, :]
        xr = ("xbuf", b, tg)
        S.add("act", t_act(1024),
              lambda a, xg=xg, tg=tg: a.activation(out=abf[tg][:, :], in_=xg, func=AF.Square, scale=1.0 / 32.0,
                                                   accum_out=col(C_SSX, tg)),
              reads=[xr], writes=[("st", C_SSX + tg), ("abf", tg)], name=f"X1_{k}_{tg}")
        rstd_ops(C_SSX, C_EX, C_RX, tg, f"x{k}_{tg}")
        ab = abf[tg]
        S.add("act", t_act(1024) + 0.1,
              lambda a, xg=xg, ab=ab, tg=tg: a.activation(out=ab[:, :], in_=xg, func=AF.Copy, scale=col(C_RX, tg)),
              reads=[xr, ("st", C_RX + tg)], writes=[("abf", tg)], name=f"X3_{k}_{tg}")

    def pe_transposes(src_bf, bk, rows=128):
        def f(t):
            ins = None
            for kc in range(8):
                ins = t.transpose(bank_bf(bk)[:, kc, 0:rows], src_bf[0:rows, kc * 128:(kc + 1) * 128],
                                  ident[0:rows, 0:rows])
            return ins
        return f

    def phase_Ta(k, tg):
        bk = tg
        S.add("pe", 8 * TP, pe_transposes(abf[tg], bk), reads=[("abf", tg), "ident"],
              writes=[("bank", bk)], name=f"Ta_{k}_{tg}")
        S.add("act", t_act(1024),
              lambda a, bk=bk, tg=tg: a.activation(out=aT[:, :, tg * 128:(tg + 1) * 128], in_=bank_bf(bk),
                                                   func=AF.Copy),
              reads=[("bank", bk)], writes=[("aT", tg)], name=f"X4_{k}_{tg}")

    in_n = [0]

    def pe_in(slot, bk, rhs, ncol):
        def f(t):
            ins = None
            for kc in range(8):
                ins = t.matmul(bank(bk)[:, 0:ncol], ring[:, slot, kc * 128:(kc + 1) * 128], rhs[:, kc, :],
                               start=(kc == 0), stop=(kc == 7))
            return ins
        return f

    def halo_ops(k):
        first = (k % nt == 0)
        if first:
            S.add("pool", 0.2, lambda g: g.tensor_copy(out=ubuf[:, :, 0:2], in_=umeta[:, :, NMETA - 2:NMETA]),
                  reads=["umeta"], writes=[("u", q) for q in range(4)], name=f"halo_u{k}")
            S.add("pool", 0.2, lambda g: g.tensor_copy(out=psb[:, :, 1:16], in_=pmeta[:, :, 1:16]),
                  reads=["pmeta"], writes=[("p", q) for q in range(4)], name=f"halo_p{k}")
        else:
            S.add("pool", 0.2, lambda g: g.tensor_copy(out=ubuf[:, :, 0:2], in_=ubuf[:, :, TT:TT + 2]),
                  reads=[("u", q) for q in range(4)], writes=[("u", q) for q in range(4)], name=f"halo_u{k}")
            S.add("pool", 0.2, lambda g: g.tensor_copy(out=psb[:, :, 1:16], in_=psb[:, :, TT + 1:TT + 16]),
                  reads=[("p", q) for q in range(4)], writes=[("p", q) for q in range(4)], name=f"halo_p{k}")

    def z_conv_pre(k, q, bc, bv):
        ucur = ubuf[:, q, 2:TT + 2]
        ur = ("u", q)
        S.add("act", t_act(TT), lambda a, ucur=ucur, bc=bc: a.activation(out=ucur, in_=bank(bc), func=AF.Copy),
              reads=[("bank", bc)], writes=[ur], name=f"Z1_{k}_{q}")
        S.add("dve", t_dve(TT, psum=True),
              lambda v, ucur=ucur, bv=bv: v.tensor_tensor(out=ucur, in0=ucur, in1=bank(bv), op=ALU.mult),
              reads=[ur, ("bank", bv)], writes=[ur], name=f"Z2_{k}_{q}")
        a1, a2 = t1[q % 2], t2[q % 2]
        S.add("act", t_act(TT) + 0.1,
              lambda a, ucur=ucur, a1=a1, q=q: a.activation(out=a1[:, :], in_=ucur, func=AF.Copy,
                                                            scale=convw_col[:, 2, q:q + 1]),
              reads=[ur, "convw_col"], writes=[("t1", q % 2)], name=f"Z3_{k}_{q}")
        S.add("dve", t_dve(TT),
              lambda v, a1=a1, a2=a2, q=q: v.scalar_tensor_tensor(out=a2[:, :], in0=ubuf[:, q, 1:TT + 1],
                                                                  scalar=convw_col[:, 1, q:q + 1], in1=a1[:, :],
                                                                  op0=ALU.mult, op1=ALU.add),
              reads=[ur, ("t1", q % 2), "convw_col"], writes=[("t2", q % 2)], name=f"Z4_{k}_{q}")
        S.add("dve", t_dve(TT),
              lambda v, a1=a1, a2=a2, q=q: v.scalar_tensor_tensor(out=a1[:, :], in0=ubuf[:, q, 0:TT],
                                                                  scalar=convw_col[:, 0, q:q + 1], in1=a2[:, :],
                                                                  op0=ALU.mult, op1=ALU.add),
              reads=[ur, ("t2", q % 2), "convw_col"], writes=[("t1", q % 2)], name=f"Z5_{k}_{q}")

    def z_conv_post(k, q, bb):
        a1 = t1[q % 2]
        S.add("dve", t_dve(TT, psum=True),
              lambda v, a1=a1, q=q, bb=bb: v.tensor_tensor(out=ybf[:, q, :], in0=a1[:, :], in1=bank(bb), op=ALU.mult),
              reads=[("t1", q % 2), ("bank", bb)], writes=[("y", q)], name=f"Z6_{k}_{q}")

    def z_pool_ops(k, g, bp):
        pr = ("p", g)
        W = WINS[g]
        S.add("act", t_act(TT), lambda a, g=g, bp=bp: a.activation(out=psb[:, g, 16:TT + 16], in_=bank(bp), func=AF.Copy),
              reads=[("bank", bp)], writes=[pr], name=f"P1_{k}_{g}")
        A, B = wsA[0], wsB[0]
        ra, rb = ("wsA", 0), ("wsB", 0)
        src = psb[:, g, :]
        sres = pr
        cur, cres = None, None
        w = 1
        bufs = [(A, ra), (B, rb)]
        bi = 0
        while w < W:
            lo = 16 - (W - 2 * w)
            dstb, dres = bufs[bi]
            bi ^= 1
            if cur is None:
                i0, i1 = src[:, lo:TT + 16], src[:, lo - w:TT + 16 - w]
                rd = [sres]
            else:
                i0, i1 = cur[:, lo:TT + 16], cur[:, lo - w:TT + 16 - w]
                rd = [cres]
            S.add("dve", t_dve(TT + 16 - lo),
                  lambda v, o=dstb[:, lo:TT + 16], i0=i0, i1=i1: v.tensor_tensor(out=o, in0=i0, in1=i1, op=ALU.add),
                  reads=rd, writes=[dres], name=f"P2_{k}_{g}_{w}")
            cur, cres = dstb, dres
            w *= 2
        S.add("dve", t_dve(TT),
              lambda v, cur=cur, g=g, W=W: v.scalar_tensor_tensor(out=pooled[:, g, :], in0=cur[:, 16:TT + 16],
                                                                  scalar=1.0 / W, in1=psb[:, g, 16:TT + 16],
                                                                  op0=ALU.mult, op1=ALU.subtract),
              reads=[cres, pr], writes=[("y", 4 + g)], name=f"P3_{k}_{g}")

    def phase_IN(k, mid=None):
        halo_ops(k)
        banks = {}
        for n, j in enumerate(_DBG.get("in_order", IN_ORDER)):
            if n == 8 and mid is not None:
                mid()
            slot = ring_load("in", j)
            bk = in_n[0] % 4
            in_n[0] += 1
            banks[j] = bk
            S.add("pe", 8 * MM, pe_in(slot, bk, aT, TT), reads=[("ring", slot)] + [("aT", g) for g in range(4)],
                  writes=[("bank", bk)], name=f"IN_{k}_{j}")
            if j < 4:
                z_conv_post(k, j, bk)
            elif 8 <= j < 12:
                z_conv_pre(k, j - 8, banks[j - 4], bk)
            elif j >= 12:
                z_pool_ops(k, j - 12, bk)

    def phase_POOLMM(k):
        for g in range(4):
            bk = g
            S.add("pe", MM, lambda t, g=g, bk=bk: t.matmul(bank(bk), poolw_bf[:, g, :], pooled[:, g, :], start=True, stop=True),
                  reads=["poolw_bf", ("y", 4 + g)], writes=[("bank", bk)], name=f"PM_{k}_{g}")
            S.add("act", t_act(TT) + 0.1,
                  lambda a, g=g, bk=bk: a.activation(out=ybf[:, 4 + g, :], in_=bank(bk), func=AF.Copy,
                                                     scale=pscale_col[:, g:g + 1]),
                  reads=[("bank", bk), "pscale_col"], writes=[("y", 4 + g)], name=f"P4_{k}_{g}")

    def phase_OUT(k):
        slots = [ring_load("out", kc) for kc in range(8)]
        b = k % 3
        for tg in range(4):
            pi = tg % 2

            def f(t, tg=tg, pi=pi):
                ins = None
                for kc in range(8):
                    for nh in range(2):
                        ins = t.matmul(pp[pi][:, nh * 512:(nh + 1) * 512], ybf[:, kc, tg * 128:(tg + 1) * 128],
                                       ring[:, slots[kc], nh * 512:(nh + 1) * 512], start=(kc == 0), stop=(kc == 7))
                return ins
            bks = [("bank", 2 * pi), ("bank", 2 * pi + 1)]
            S.add("pe", 16 * MM, f, reads=[("ring", s) for s in slots] + [("y", c) for c in range(8)],
                  writes=bks, name=f"OUT_{k}_{tg}")
            xg = xbuf[b][:, tg, :]
            xr = ("xbuf", b, tg)
            S.add("act", t_act(1024),
                  lambda a, pi=pi, tg=tg: a.activation(out=tt[tg % 2][:, :], in_=pair(pi), func=AF.Square, scale=1.0 / 32.0,
                                                       accum_out=col(C_SSM, tg)),
                  reads=bks, writes=[("st", C_SSM + tg), ("tt", tg % 2)], name=f"M1_{k}_{tg}")
            rstd_ops(C_SSM, C_EM, C_RM, tg, f"m{k}_{tg}")
            tb = tt[tg % 2]
            S.add("dve", t_dve(1024, psum=True),
                  lambda v, pi=pi, tb=tb: v.tensor_tensor(out=tb[:, :], in0=pair(pi), in1=gpost_b[:, :], op=ALU.mult),
                  reads=bks + ["gpost_b", ("st", C_SSM + tg)], writes=[("tt", tg % 2)], name=f"M3_{k}_{tg}")
            S.add("dve", t_dve(1024),
                  lambda v, xg=xg, tb=tb, tg=tg: v.scalar_tensor_tensor(out=xg, in0=tb[:, :], scalar=col(C_RM, tg), in1=xg,
                                                                        op0=ALU.mult, op1=ALU.add),
                  reads=[("tt", tg % 2), ("st", C_RM + tg), xr], writes=[xr], name=f"M4_{k}_{tg}")
            S.add("act", t_act(1024),
                  lambda a, xg=xg, tg=tg: a.activation(out=fbf[tg][:, :], in_=xg, func=AF.Square, scale=1.0 / 32.0,
                                                       accum_out=col(C_SSH, tg)),
                  reads=[xr], writes=[("st", C_SSH + tg), ("fbf", tg)], name=f"M5_{k}_{tg}")
            rstd_ops(C_SSH, C_EH, C_RH, tg, f"h{k}_{tg}")
            fb = fbf[tg]
            S.add("act", t_act(1024) + 0.1,
                  lambda a, xg=xg, fb=fb, tg=tg: a.activation(out=fb[:, :], in_=xg, func=AF.Copy, scale=col(C_RH, tg)),
                  reads=[xr, ("st", C_RH + tg)], writes=[("fbf", tg)], name=f"M7_{k}_{tg}")

    def phase_Tf(k):
        for tg in range(4):
            bk = 4 + tg
            S.add("pe", 8 * TP, pe_transposes(fbf[tg], bk), reads=[("fbf", tg), "ident"],
                  writes=[("bank", bk)], name=f"Tf_{k}_{tg}")
            S.add("act", t_act(1024),
                  lambda a, bk=bk, tg=tg: a.activation(out=fT[:, :, tg * 128:(tg + 1) * 128], in_=bank_bf(bk),
                                                       func=AF.Copy),
                  reads=[("bank", bk)], writes=[("fT", tg)], name=f"M8_{k}_{tg}")

    def phase_GU(k, mid=None, late=None):
        for j in range(NFF):
            if j == NFF // 2 and mid is not None:
                mid()
            if j == 16 and late is not None:
                late()
            bg, bu = 4 + 2 * (j % 2), 5 + 2 * (j % 2)
            for which, bk in ((0, bg), (1, bu)):
                slot = ring_load("gu", 2 * j + which)
                S.add("pe", 8 * MM, pe_in(slot, bk, fT, TT), reads=[("ring", slot)] + [("fT", g) for g in range(4)],
                      writes=[("bank", bk)], name=f"GU_{k}_{j}_{which}")
            sl = sil[j % 2]
            S.add("act", t_act(TT), lambda a, sl=sl, bg=bg: a.activation(out=sl[:, :], in_=bank(bg), func=AF.Silu),
                  reads=[("bank", bg)], writes=[("sil", j % 2)], name=f"G1_{k}_{j}")
            S.add("dve", t_dve(TT, psum=True),
                  lambda v, sl=sl, bu=bu, j=j: v.tensor_tensor(out=gbf[:, j, :], in0=sl[:, :], in1=bank(bu), op=ALU.mult),
                  reads=[("sil", j % 2), ("bank", bu)], writes=[("g", j)], name=f"G2_{k}_{j}")

    DN_PAIR = {0: 2, 1: 3, 2: 0, 3: 1}
    JB = 4

    def phase_DOWN(k):
        b = k % 3
        blocks = [list(range(0, 8)), list(range(8, 12)), list(range(12, 16)), list(range(16, NFF))]

        def part(j, slot, tgs, tag):
            def f(t, j=j, slot=slot, tgs=tgs):
                ins = None
                for tg in tgs:
                    for nh in range(2):
                        ins = t.matmul(pp[DN_PAIR[tg]][:, nh * 512:(nh + 1) * 512], gbf[:, j, tg * 128:(tg + 1) * 128],
                                       ring[:, slot, nh * 512:(nh + 1) * 512], start=(j == 0), stop=(j == NFF - 1))
                return ins
            bks = []
            for tg in tgs:
                bks += [("bank", 2 * DN_PAIR[tg]), ("bank", 2 * DN_PAIR[tg] + 1)]
            S.add("pe", 4 * MM, f, reads=[("ring", slot), ("g", j)], writes=bks, name=f"DN_{k}_{tag}_{j}")

        for bi, blk in enumerate(blocks):
            slots = {j: ring_load("dn", j) for j in blk}
            order_ = (("B", (0, 1)), ("A", (2, 3)))
            if bi == len(blocks) - 1:
                order_ = order_[::-1]
            for tag, tgs in order_:
                for j in blk:
                    part(j, slots[j], tgs, tag)
        for tgs in ((2, 3), (0, 1)):
            for tg in tgs:
                pi = DN_PAIR[tg]
                pb = [("bank", 2 * pi), ("bank", 2 * pi + 1)]
                xg = xbuf[b][:, tg, :]
                xr = ("xbuf", b, tg)
                S.add("act", t_act(1024),
                      lambda a, pi=pi, tg=tg: a.activation(out=tt[tg % 2][:, :], in_=pair(pi), func=AF.Square, scale=1.0 / 32.0,
                                                           accum_out=col(C_SSD, tg)),
                      reads=pb, writes=[("st", C_SSD + tg), ("tt", tg % 2)], name=f"D1_{k}_{tg}")
                rstd_ops(C_SSD, C_ED, C_RD, tg, f"d{k}_{tg}")
                tb = tt[tg % 2]
                S.add("dve", t_dve(1024, psum=True),
                      lambda v, pi=pi, tb=tb: v.tensor_tensor(out=tb[:, :], in0=pair(pi), in1=gfpost_b[:, :], op=ALU.mult),
                      reads=pb + ["gfpost_b", ("st", C_SSD + tg)], writes=[("tt", tg % 2)], name=f"D3_{k}_{tg}")
                S.add("dve", t_dve(1024),
                      lambda v, xg=xg, tb=tb, tg=tg: v.scalar_tensor_tensor(out=xg, in0=tb[:, :], scalar=col(C_RD, tg), in1=xg,
                                                                            op0=ALU.mult, op1=ALU.add),
                      reads=[("tt", tg % 2), ("st", C_RD + tg), xr], writes=[xr], name=f"D4_{k}_{tg}")
            store_out(k, list(tgs))

    def phase_meta_load():
        mb = tt[0]
        key = ("meta",)
        dsem(key)

        def ldm(sp):
            return sp.dma_start(out=mb[0:NMETA, :], in_=meta_d[:, :]).then_inc(dma_sem[key], 16)
        S.add("sp", 0.1, ldm, writes=[("tt", 0)], dma=(key, 64e3), name="ld_meta")

    def phase_meta():
        mb = tt[0]
        S.add("act", t_act(1024),
              lambda a: a.activation(out=abf[0][0:NMETA, :], in_=mb[0:NMETA, :], func=AF.Square, scale=1.0 / 32.0,
                                     accum_out=st[0:NMETA, C_SSX:C_SSX + 1]),
              reads=[("tt", 0)], writes=[("st", C_SSX), ("abf", 0)], name="X1_meta")
        S.add("pool", 0.15, lambda g: g.tensor_scalar(out=st[0:NMETA, C_EX:C_EX + 1], in0=st[0:NMETA, C_SSX:C_SSX + 1],
                                                      scalar1=EPS, scalar2=None, op0=ALU.add),
              reads=[("st", C_SSX)], writes=[("st", C_EX)], name="eps_meta")
        S.add("pool", 0.15, lambda g: g.tensor_tensor(out=st[0:NMETA, C_RX:C_RX + 1], in0=st[0:NMETA, C_EX:C_EX + 1],
                                                      in1=nhalf[0:NMETA, :], op=ALU.pow),
              reads=[("st", C_EX), "nhalf"], writes=[("st", C_RX)], name="pow_meta")
        S.add("act", 1.2, lambda a: a.activation(out=abf[0][0:NMETA, :], in_=mb[0:NMETA, :], func=AF.Copy,
                                                 scale=st[0:NMETA, C_RX:C_RX + 1]),
              reads=[("tt", 0), ("st", C_RX)], writes=[("abf", 0)], name="X3_meta")
        S.add("pe", 8 * TP, pe_transposes(abf[0], 0, rows=NMETA), reads=[("abf", 0), "ident"], writes=[("bank", 0)],
              name="Ta_meta")
        S.add("act", 0.4, lambda a: a.activation(out=aTm[:, :, :], in_=bank_bf(0)[:, :, 0:NMETA], func=AF.Copy),
              reads=[("bank", 0)], writes=["aTm"], name="X4_meta")
        banks = {}
        for n, j in enumerate(_DBG.get("in_order", IN_ORDER)):
            if j < 4:
                continue
            slot = ring_load("in", j)
            bk = in_n[0] % 4
            in_n[0] += 1
            banks[j] = bk
            S.add("pe", 8 * 0.05, pe_in(slot, bk, aTm, NMETA), reads=[("ring", slot), "aTm"], writes=[("bank", bk)],
                  name=f"INm_{j}")
            if 8 <= j < 12:
                q = j - 8
                bc = banks[4 + q]
                S.add("act", 0.3, lambda a, q=q, bc=bc: a.activation(out=umeta[:, q, :], in_=bank(bc)[:, 0:NMETA], func=AF.Copy),
                      reads=[("bank", bc)], writes=[("um", q)], name=f"Z1m_{q}")
                S.add("dve", 0.2, lambda v, q=q, bk=bk: v.tensor_tensor(out=umeta[:, q, :], in0=umeta[:, q, :],
                                                                        in1=bank(bk)[:, 0:NMETA], op=ALU.mult),
                      reads=[("um", q), ("bank", bk)], writes=[("um", q), "umeta"], name=f"Z2m_{q}")
            elif j >= 12:
                g = j - 12
                S.add("act", 0.3, lambda a, g=g, bk=bk: a.activation(out=pmeta[:, g, :], in_=bank(bk)[:, 0:NMETA], func=AF.Copy),
                      reads=[("bank", bk)], writes=[("pm", g), "pmeta"], name=f"P1m_{g}")

    const_setup()
    phase_meta()
    const_setup2()
    for k in range(min(3, ntile)):
        load_x(k)
    def x_and_ta(k):
        if k < ntile:
            for tg in range(4):
                phase_X(k, tg)
                phase_Ta(k, tg)

    x_and_ta(0)
    phase_IN(0)
    phase_POOLMM(0)
    x_and_ta(1)
    phase_OUT(0)
    for k in range(ntile):
        if k + 1 < ntile:
            phase_IN(k + 1, mid=lambda k=k: phase_Tf(k))
            phase_GU(k, mid=lambda k=k: phase_POOLMM(k + 1), late=lambda k=k: x_and_ta(k + 2))
            phase_OUT(k + 1)
        else:
            phase_Tf(k)
            phase_GU(k)
        phase_DOWN(k)
        if k + 3 < ntile:
            load_x(k + 3)

    if debug:
        dbg_d = nc.dram_tensor("dbg", [128, 64], F32, kind="ExternalOutput").ap()
        dsem(("dbg",))

        def dbgf(sp):
            return sp.dma_start(out=dbg_d[:, :], in_=st[:, :]).then_inc(dma_sem[("dbg",)], 16)
        S.add("sp", 0.1, dbgf, reads=[("st", c) for c in range(48)] + [("outdone", ntile - 1, 0)],
              writes=[("outdone", "dbg", 0)], dma=(("dbg",), 32e3), name="st_dbg")
    order = S.simulate()
    if verbose:
        print(f"[sched] ops={len(S.ops)} sim_time={S.sim_time:.1f}us "
              + " ".join(f"{e}={len(order[e])}" for e in order))

    ops = S.ops
    for e in ("pe", "act", "dve", "pool"):
        n = 0
        for i in order[e]:
            if ops[i].dma is None:
                n += 1
                ops[i].sig = (eng_sem[e], n)
    cnt = {}
    for op in ops:
        if op.dma is not None:
            key = op.dma[0]
            cnt[key] = cnt.get(key, 0) + 16
            op.sig = (dma_sem[key], cnt[key])

    finals = [i for i, op in enumerate(ops) if op.dma is not None and op.name.startswith("st")]

    def emit_engine(e, eng):
        seen = {}
        for i in order[e]:
            op = ops[i]
            for d in op.deps:
                dop = ops[d]
                if e == "pe" and dop.eng == "pe":
                    continue
                sem, val = dop.sig
                key = id(sem)
                if seen.get(key, 0) >= val:
                    continue
                seen[key] = val
                eng.wait_ge(sem, val)
            ins = op.emit(eng)
            if op.dma is None:
                ins.then_inc(eng_sem[e], 1)
        if e == "sp":
            for i in finals:
                sem, val = ops[i].sig
                eng.wait_ge(sem, val)

    block = es.enter_context(nc.Block())

    @block.sync
    def _(sp):
        emit_engine("sp", sp)

    @block.tensor
    def _(t):
        emit_engine("pe", t)

    @block.scalar
    def _(a):
        emit_engine("act", a)

    @block.vector
    def _(v):
        emit_engine("dve", v)

    @block.gpsimd
    def _(g):
        emit_engine("pool", g)

    es.close()
    return nc, S


_CACHE = {}


def _get_program(nseq, nt):
    key = (nseq, nt)
    if key not in _CACHE:
        _CACHE[key] = build_program(nseq, nt)[0]
    return _CACHE[key]


def kernel(x, meta_tokens, norm_mix_pre, w_in, conv_w, pool_w, pool_scale, w_out, norm_mix_post,
           norm_ffn_pre, w_gate, w_up, w_down, norm_ffn_post):
    x = np.asarray(x)
    B, SEQ, _ = x.shape
    ncores = 8
    nseq = B // ncores
    nt = SEQ // TT
    nc = _get_program(nseq, nt)
    f = lambda a: np.ascontiguousarray(np.asarray(a, dtype=np.float32))
    shared = {
        "meta_tokens": f(meta_tokens),
        "norm_mix_pre": f(norm_mix_pre).reshape(D),
        "w_in": f(w_in).reshape(D, 2 * D),
        "conv_w": f(conv_w).reshape(3, 512),
        "pool_w": f(pool_w).reshape(4, 128, 128),
        "pool_scale": f(pool_scale).reshape(512),
        "w_out": f(w_out).reshape(D, D),
        "norm_mix_post": f(norm_mix_post).reshape(D),
        "norm_ffn_pre": f(norm_ffn_pre).reshape(D),
        "w_gate": f(w_gate).reshape(D, DFF),
        "w_up": f(w_up).reshape(D, DFF),
        "w_down": f(w_down).reshape(DFF, D),
        "norm_ffn_post": f(norm_ffn_post).reshape(D),
    }
    xs = f(x).reshape(ncores, nseq * SEQ, D)
    in_maps = [dict(shared, x=xs[c]) for c in range(ncores)]
    res = run_bass_kernel_spmd(nc, in_maps, core_ids=list(range(ncores)))
    out = np.stack([np.asarray(r["out"]) for r in res.results], axis=0)
    return out.reshape(B, SEQ, D).astype(np.float32, copy=False)
```

```python
import heapq
from contextlib import ExitStack

import numpy as np
import concourse.bass as bass
import concourse.mybir as mybir
from concourse.bass_utils import run_bass_kernel_spmd

F32 = mybir.dt.float32
BF16 = mybir.dt.bfloat16
AF = mybir.ActivationFunctionType
ALU = mybir.AluOpType

D = 1024
DFF = 2816
NFF = DFF // 128
NMETA = 16
TT = 512
RING = 16
EPS = 1e-6
WINS = (2, 4, 8, 16)
IN_ORDER = [4, 8, 5, 9, 0, 6, 10, 1, 7, 11, 2, 12, 13, 3, 14, 15]

MM = 0.218
TP = 0.068


def t_act(n):
    return 0.22 + n / 1200.0


def t_dve(n, psum=False, fast=False):
    return 0.13 + (0.06 if psum else 0.0) + n / (1640.0 if fast else 960.0)


SYNC_LAT = 0.12


_DBG = {}


class Op:
    __slots__ = ("idx", "eng", "dur", "emit", "deps", "dma", "name", "users", "start", "finish",
                 "sig", "nrem", "delay")

    def __init__(self, idx, eng, dur, emit, deps, dma, name):
        self.idx = idx
        self.eng = eng
        self.dur = dur
        self.emit = emit
        self.deps = deps
        self.dma = dma
        self.name = name
        self.users = []
        self.start = None
        self.finish = None
        self.sig = None
        self.delay = 0.0


class Sched:
    FIFO = ("pe",)

    def __init__(self):
        self.ops = []
        self.last_writer = {}
        self.readers = {}

    def add(self, eng, dur, emit, reads=(), writes=(), dma=None, name="", delay=0.0):
        idx = len(self.ops)
        deps = set()
        for r in reads:
            w = self.last_writer.get(r)
            if w is not None:
                deps.add(w)
        for w_ in writes:
            w = self.last_writer.get(w_)
            if w is not None:
                deps.add(w)
            for rd in self.readers.get(w_, ()):
                deps.add(rd)
        for r in reads:
            self.readers.setdefault(r, []).append(idx)
        for w_ in writes:
            self.last_writer[w_] = idx
            self.readers[w_] = []
        deps.discard(idx)
        op = Op(idx, eng, dur, emit, sorted(deps), dma, name)
        op.delay = 0.0 if _DBG.get("no_delay") else delay
        self.ops.append(op)
        return idx

    def simulate(self):
        ops = self.ops
        for op in ops:
            op.nrem = len(op.deps)
            for d in op.deps:
                ops[d].users.append(op.idx)
        engines = ("pe", "act", "dve", "pool", "sp")
        ready = {e: [] for e in engines}
        busy = {e: False for e in engines}
        order = {e: [] for e in engines}
        fifo_next = {e: 0 for e in engines}
        fifo_list = {e: [op.idx for op in ops if op.eng == e] for e in self.FIFO}
        ev = []
        seq = [0]
        dma_pipe = [0.0]

        def push(t, kind, x):
            seq[0] += 1
            heapq.heappush(ev, (t, seq[0], kind, x))

        def try_start(e, t):
            if busy[e] or not ready[e]:
                return
            if e in self.FIFO:
                want = fifo_list[e][fifo_next[e]]
                if ready[e][0] != want:
                    return
                fifo_next[e] += 1
            i = heapq.heappop(ready[e])
            op = ops[i]
            op.start = t
            order[e].append(i)
            busy[e] = True
            if op.dma is not None:
                push(t + op.dur, "free", e)
                b = op.dma[1]
                dma_pipe[0] = max(dma_pipe[0], t + 0.1) + b / 260e3
                push(dma_pipe[0] + 2.0, "fin", i)
            else:
                push(t + op.dur, "free", e)
                push(t + op.dur, "fin", i)

        for op in ops:
            if op.nrem == 0:
                heapq.heappush(ready[op.eng], op.idx)
        for e in engines:
            try_start(e, 0.0)
        tmax = 0.0
        while ev:
            t, _, kind, x = heapq.heappop(ev)
            tmax = max(tmax, t)
            if kind == "free":
                busy[x] = False
                try_start(x, t)
            elif kind == "rdy":
                heapq.heappush(ready[ops[x].eng], x)
                try_start(ops[x].eng, t)
            else:
                op = ops[x]
                op.finish = t
                for u in op.users:
                    if ops[u].eng != op.eng and ops[u].delay == 0.0:
                        ops[u].delay = -SYNC_LAT
                    uo = ops[u]
                    uo.nrem -= 1
                    if uo.nrem == 0:
                        if uo.delay != 0:
                            push(t + abs(uo.delay), "rdy", u)
                        else:
                            heapq.heappush(ready[uo.eng], u)
                            try_start(uo.eng, t)
        undone = [op.name for op in ops if op.start is None]
        assert not undone, f"scheduler deadlock: {undone[:10]}"
        self.order = order
        self.sim_time = tmax
        return order


def build_program(nseq, nt, verbose=False, debug=False):
    ntile = nseq * nt
    ntok = ntile * TT
    nc = bass.Bass("TRN2", target_bir_lowering=False)
    x_d = nc.dram_tensor("x", [ntok, D], F32, kind="ExternalInput").ap()
    meta_d = nc.dram_tensor("meta_tokens", [NMETA, D], F32, kind="ExternalInput").ap()
    g_pre_d = nc.dram_tensor("norm_mix_pre", [D], F32, kind="ExternalInput").ap()
    w_in_d = nc.dram_tensor("w_in", [D, 2 * D], F32, kind="ExternalInput").ap()
    conv_w_d = nc.dram_tensor("conv_w", [3, 512], F32, kind="ExternalInput").ap()
    pool_w_d = nc.dram_tensor("pool_w", [4, 128, 128], F32, kind="ExternalInput").ap()
    pool_s_d = nc.dram_tensor("pool_scale", [512], F32, kind="ExternalInput").ap()
    w_out_d = nc.dram_tensor("w_out", [D, D], F32, kind="ExternalInput").ap()
    g_post_d = nc.dram_tensor("norm_mix_post", [D], F32, kind="ExternalInput").ap()
    g_ffn_d = nc.dram_tensor("norm_ffn_pre", [D], F32, kind="ExternalInput").ap()
    w_gate_d = nc.dram_tensor("w_gate", [D, DFF], F32, kind="ExternalInput").ap()
    w_up_d = nc.dram_tensor("w_up", [D, DFF], F32, kind="ExternalInput").ap()
    w_down_d = nc.dram_tensor("w_down", [DFF, D], F32, kind="ExternalInput").ap()
    g_fpost_d = nc.dram_tensor("norm_ffn_post", [D], F32, kind="ExternalInput").ap()
    out_d = nc.dram_tensor("out", [ntok, D], F32, kind="ExternalOutput").ap()
    scr = {
        "in": nc.dram_tensor("scr_in", [16, 128, 1024], BF16).ap(),
        "out": nc.dram_tensor("scr_out", [8, 128, 1024], BF16).ap(),
        "gu": nc.dram_tensor("scr_gu", [2 * NFF, 128, 1024], BF16).ap(),
        "dn": nc.dram_tensor("scr_dn", [NFF, 128, 1024], BF16).ap(),
    }

    S = Sched()
    es = ExitStack()

    def sb(name, shape, dt):
        return es.enter_context(nc.sbuf_tensor(name, shape, dt))

    xbuf = [sb(f"xbuf{i}", [128, 4, D], F32) for i in range(3)]
    ring = sb("ring", [128, RING, 1024], BF16)
    NSTAGE = 3
    stage = [sb(f"stage{i}", [128, 1024], F32) for i in range(NSTAGE)]
    abf = [sb(f"abf{i}", [128, D], BF16) for i in range(4)]
    fbf = [sb(f"fbf{i}", [128, D], BF16) for i in range(4)]
    aT = sb("aT", [128, 8, TT], BF16)
    fT = sb("fT", [128, 8, TT], BF16)
    aTm = sb("aTm", [128, 8, NMETA], BF16)
    ubuf = sb("ubuf", [128, 4, TT + 2], F32)
    psb = sb("psb", [128, 4, TT + 16], F32)
    umeta = sb("umeta", [128, 4, NMETA], F32)
    pmeta = sb("pmeta", [128, 4, NMETA], F32)
    t1 = [sb(f"t1_{i}", [128, TT], F32) for i in range(2)]
    t2 = [sb(f"t2_{i}", [128, TT], F32) for i in range(2)]
    wsA = [sb("wsA0", [128, TT + 16], F32)]
    wsB = [sb("wsB0", [128, TT + 16], F32)]
    ybf = sb("ybf", [128, 8, TT], BF16)
    pooled = ybf[:, 4:8, :]
    sil = [sb(f"sil{i}", [128, TT], F32) for i in range(2)]
    gbf = sb("gbf", [128, NFF, TT], BF16)
    tt = [sb(f"tt{i}", [128, D], F32) for i in range(2)]
    gpost_b = sb("gpost_b", [128, D], F32)
    gfpost_b = sb("gfpost_b", [128, D], F32)
    ident = sb("ident", [128, 128], BF16)
    poolw_bf = sb("poolw_bf", [128, 4, 128], BF16)
    gpre_col = sb("gpre_col", [128, 8], F32)
    gffn_col = sb("gffn_col", [128, 8], F32)
    convw_col = sb("convw_col", [128, 3, 4], F32)
    pscale_col = sb("pscale_col", [128, 4], F32)
    nhalf = sb("nhalf", [128, 1], F32)
    st = sb("stats", [128, 64], F32)
    pp = [es.enter_context(nc.psum_tensor(f"pp{i}", [128, 1024], F32)) for i in range(4)]

    def bank(i):
        return pp[i // 2][:, (i % 2) * 512:(i % 2) * 512 + 512]

    def bank_bf(i):
        return bank(i).bitcast(BF16).rearrange("p (k m) -> p k m", k=8)

    def pair(i):
        return pp[i][:, :]

    eng_sem = {e: es.enter_context(nc.semaphore(f"sem_{e}")) for e in ("pe", "act", "dve", "pool")}
    dma_sem = {}
    dma_cnt = {}

    def dsem(key):
        if key not in dma_sem:
            dma_sem[key] = es.enter_context(nc.semaphore("dma_" + "_".join(str(k) for k in key)))
            dma_cnt[key] = 0
        return dma_sem[key]

    def col(base, i):
        return st[:, base + i:base + i + 1]

    C_SSX, C_EX, C_RX = 0, 4, 8
    C_SSM, C_EM, C_RM = 12, 16, 20
    C_SSH, C_EH, C_RH = 24, 28, 32
    C_SSD, C_ED, C_RD = 36, 40, 44

    def rstd_ops(c_ss, c_e, c_r, i, tag):
        S.add("pool", 0.25,
              lambda g, a=col(c_e, i), b=col(c_ss, i): g.tensor_scalar(
                  out=a, in0=b, scalar1=EPS, scalar2=None, op0=ALU.add),
              reads=[("st", c_ss + i)], writes=[("st", c_e + i)], name=f"eps{tag}")
        S.add("pool", 0.6,
              lambda g, a=col(c_r, i), b=col(c_e, i): g.tensor_tensor(
                  out=a, in0=b, in1=nhalf[:], op=ALU.pow),
              reads=[("st", c_e + i), "nhalf"], writes=[("st", c_r + i)], name=f"pow{tag}")

    cloads = []

    def const_setup():
        phase_meta_load()

        S.add("pool", 0.2, lambda g: g.memset(ident[:], 0.0), writes=["ident"], name="ident0")
        S.add("pool", 0.3,
              lambda g: g.affine_select(out=ident[:], in_=ident[:], compare_op=ALU.not_equal, fill=1.0,
                                        base=0, pattern=[[-1, 128]], channel_multiplier=1),
              reads=["ident"], writes=["ident"], name="ident")
        S.add("pool", 0.1, lambda g: g.memset(nhalf[:], -0.5), writes=["nhalf"], name="nhalf")

        def cload(name, out_ap, in_ap, res, strided, after=()):
            key = ("c", name)
            dsem(key)

            def f(sp, out_ap=out_ap, in_ap=in_ap, key=key, strided=strided):
                if strided:
                    with nc.allow_non_contiguous_dma(reason="tiny parameter vectors"):
                        return sp.dma_start(out=out_ap, in_=in_ap).then_inc(dma_sem[key], 16)
                return sp.dma_start(out=out_ap, in_=in_ap).then_inc(dma_sem[key], 16)
            S.add("sp", 0.1, f, reads=list(after), writes=[res], dma=(key, 64e3), name=f"c_{name}")
        cloads.append(cload)
        cload("gpre", gpre_col[:], g_pre_d.rearrange("(kc p) -> p kc", p=128), "gpre_col", True)

    def const_setup2():
        cload = cloads[0]
        cload("convw", convw_col[:], conv_w_d.rearrange("k (q p) -> p k q", p=128), "convw_col", True)
        cload("poolw", tt[1][:, 0:512].rearrange("p (g d) -> p g d", g=4), pool_w_d.rearrange("g c d -> c g d"),
              ("tt", 1), False, after=["pmeta"])
        cload("pscale", pscale_col[:], pool_s_d.rearrange("(g p) -> p g", p=128), "pscale_col", True, after=["pmeta"])
        cload("gpost", gpost_b[:], g_post_d.rearrange("(o d) -> o d", o=1).partition_broadcast(128), "gpost_b", False,
              after=["pmeta"])
        S.add("act", 0.7,
              lambda a: a.activation(out=poolw_bf[:], in_=tt[1][:, 0:512].rearrange("p (g d) -> p g d", g=4),
                                     func=AF.Copy),
              reads=[("tt", 1)], writes=["poolw_bf"], name="poolw_cast")

    def const_setup3():
        cload = cloads[0]
        cload("gffn", gffn_col[:], g_ffn_d.rearrange("(kc p) -> p kc", p=128), "gffn_col", True, after=[("y", 3)])
        cload("gfpost", gfpost_b[:], g_fpost_d.rearrange("(o d) -> o d", o=1).partition_broadcast(128), "gfpost_b", False,
              after=[("y", 3)])

    ring_n = [0]
    scr_written = set()
    stage_n = [0]

    def w_src(kind, j):
        if kind == "in":
            return w_in_d[:, j * 128:(j + 1) * 128].rearrange("(kc p) m -> p kc m", p=128), "gpre"
        if kind == "gu":
            w = w_gate_d if j % 2 == 0 else w_up_d
            c = j // 2
            return w[:, c * 128:(c + 1) * 128].rearrange("(kc p) m -> p kc m", p=128), "gffn"
        if kind == "out":
            return w_out_d[j * 128:(j + 1) * 128, :], None
        return w_down_d[j * 128:(j + 1) * 128, :], None

    def ring_load(kind, j):
        n = ring_n[0]
        ring_n[0] += 1
        s = n % RING
        rs = ("ring", s)
        dst = ring[:, s, :]
        skey = (kind, j)
        if skey in scr_written:
            k = ("ring", s)

            def ld(sp, dst=dst, src=scr[kind][j], k=k):
                dsem(k)
                dma_cnt[k] += 16
                return sp.dma_start(out=dst, in_=src).then_inc(dma_sem[k], 16)
            dsem(k)
            S.add("sp", 0.1, ld, reads=[("scr",) + skey], writes=[rs], dma=(k, 256e3), name=f"ld_{kind}{j}")
            return s
        b = stage_n[0] % NSTAGE
        stage_n[0] += 1
        src, fold = w_src(kind, j)
        k1 = ("stage", b)
        dsem(k1)
        if fold is None:
            sdst = stage[b][:, :]
        else:
            sdst = stage[b][:, :].rearrange("p (kc m) -> p kc m", kc=8)

        def ld32(sp, sdst=sdst, src=src, k1=k1):
            dma_cnt[k1] += 16
            return sp.dma_start(out=sdst, in_=src).then_inc(dma_sem[k1], 16)
        S.add("sp", 0.1, ld32, writes=[k1], dma=(k1, 512e3), name=f"ld32_{kind}{j}")
        if fold is None:
            S.add("act", t_act(1024), lambda a, dst=dst, b=b: a.activation(out=dst, in_=stage[b][:, :], func=AF.Copy),
                  reads=[k1], writes=[rs], name=f"cast_{kind}{j}")
        else:
            gcol = gpre_col if fold == "gpre" else gffn_col

            def cast(v, dst=dst, b=b, gcol=gcol):
                return v.tensor_tensor(out=dst.rearrange("p (kc m) -> p kc m", kc=8),
                                       in0=stage[b][:, :].rearrange("p (kc m) -> p kc m", kc=8),
                                       in1=gcol[:, :].unsqueeze(2).broadcast_to([128, 8, 128]),
                                       op=ALU.mult)
            S.add("dve", t_dve(1024), cast, reads=[k1, "gpre_col" if fold == "gpre" else "gffn_col"], writes=[rs], name=f"cast_{kind}{j}")
        k2 = ("scrw", s)
        dsem(k2)

        def park(sp, dst=dst, k2=k2, tgt=scr[kind][j]):
            dma_cnt[k2] += 16
            return sp.dma_start(out=tgt, in_=dst).then_inc(dma_sem[k2], 16)
        S.add("sp", 0.1, park, reads=[rs], writes=[("scr",) + skey], dma=(k2, 256e3), name=f"park_{kind}{j}")
        scr_written.add(skey)
        return s

    def x_rows(k):
        return slice(k * TT, (k + 1) * TT)

    def load_x(k):
        b = k % 3
        key = ("xld", b)
        dsem(key)

        def f(sp, b=b, k=k, key=key):
            dma_cnt[key] += 16
            return sp.dma_start(out=xbuf[b][:, :, :],
                                in_=x_d[x_rows(k), :].rearrange("(g p) d -> p g d", p=128)).then_inc(dma_sem[key], 16)
        S.add("sp", 0.1, f, writes=[("xbuf", b, g) for g in range(4)], dma=(key, 2e6), name=f"ldx{k}",
              delay=(12.0 if k >= 3 else (8.0, 70.0, 140.0)[k]))

    def store_out(k, groups):
        b = k % 3
        key = ("xst", b, groups[0])
        dsem(key)
        g0, g1 = groups[0], groups[-1] + 1

        def f(sp, b=b, k=k, key=key, g0=g0, g1=g1):
            dma_cnt[key] += 16
            r0 = k * TT + g0 * 128
            r1 = k * TT + g1 * 128
            return sp.dma_start(out=out_d[r0:r1, :].rearrange("(g p) d -> p g d", p=128),
                                in_=xbuf[b][:, g0:g1, :]).then_inc(dma_sem[key], 16)
        S.add("sp", 0.1, f, reads=[("xbuf", b, g) for g in groups], writes=[("outdone", k, groups[0])],
              dma=(key, 512e3 * len(groups)), name=f"st{k}_{groups[0]}",
              delay=(0.0 if k == ntile - 1 else 12.0))

    def phase_X(k, tg):
        b = k % 3
        xg = xbuf[b][:, tg, :]
        xr = ("xbuf", b, tg)
        S.add("act", t_act(1024),
              lambda a, xg=xg, tg=tg: a.activation(out=abf[tg][:, :], in_=xg, func=AF.Square, scale=1.0 / 32.0,
                                                   accum_out=col(C_SSX, tg)),
              reads=[xr], writes=[("st", C_SSX + tg), ("abf", tg)], name=f"X1_{k}_{tg}")
        rstd_ops(C_SSX, C_EX, C_RX, tg, f"x{k}_{tg}")
        ab = abf[tg]
        S.add("act", t_act(1024) + 0.1,
              lambda a, xg=xg, ab=ab, tg=tg: a.activation(out=ab[:, :], in_=xg, func=AF.Copy, scale=col(C_RX, tg)),
              reads=[xr, ("st", C_RX + tg)], writes=[("abf", tg)], name=f"X3_{k}_{tg}")

    def pe_transposes(src_bf, bk, rows=128):
        def f(t):
            ins = None
            for kc in range(8):
                ins = t.transpose(bank_bf(bk)[:, kc, 0:rows], src_bf[0:rows, kc * 128:(kc + 1) * 128],
                                  ident[0:rows, 0:rows])
            return ins
        return f

    def phase_Ta(k, tg):
        bk = tg
        S.add("pe", 8 * TP, pe_transposes(abf[tg], bk), reads=[("abf", tg), "ident"],
              writes=[("bank", bk)], name=f"Ta_{k}_{tg}")
        S.add("act", t_act(1024),
              lambda a, bk=bk, tg=tg: a.activation(out=aT[:, :, tg * 128:(tg + 1) * 128], in_=bank_bf(bk),
                                                   func=AF.Copy),
              reads=[("bank", bk)], writes=[("aT", tg)], name=f"X4_{k}_{tg}")

    in_n = [0]
    meta_slots = {}

    def pe_in(slot, bk, rhs, ncol):
        def f(t):
            ins = None
            for kc in range(8):
                ins = t.matmul(bank(bk)[:, 0:ncol], ring[:, slot, kc * 128:(kc + 1) * 128], rhs[:, kc, :],
                               start=(kc == 0), stop=(kc == 7))
            return ins
        return f

    def halo_ops(k):
        first = (k % nt == 0)
        if first:
            S.add("pool", 0.2, lambda g: g.tensor_copy(out=ubuf[:, :, 0:2], in_=umeta[:, :, NMETA - 2:NMETA]),
                  reads=["umeta"], writes=[("u", q) for q in range(4)], name=f"halo_u{k}")
            S.add("pool", 0.2, lambda g: g.tensor_copy(out=psb[:, :, 1:16], in_=pmeta[:, :, 1:16]),
                  reads=["pmeta"], writes=[("p", q) for q in range(4)], name=f"halo_p{k}")
        else:
            S.add("pool", 0.2, lambda g: g.tensor_copy(out=ubuf[:, :, 0:2], in_=ubuf[:, :, TT:TT + 2]),
                  reads=[("u", q) for q in range(4)], writes=[("u", q) for q in range(4)], name=f"halo_u{k}")
            S.add("pool", 0.2, lambda g: g.tensor_copy(out=psb[:, :, 1:16], in_=psb[:, :, TT + 1:TT + 16]),
                  reads=[("p", q) for q in range(4)], writes=[("p", q) for q in range(4)], name=f"halo_p{k}")

    def z_conv_pre(k, q, bc, bv):
        ucur = ubuf[:, q, 2:TT + 2]
        ur = ("u", q)
        S.add("act", t_act(TT), lambda a, ucur=ucur, bc=bc: a.activation(out=ucur, in_=bank(bc), func=AF.Copy),
              reads=[("bank", bc)], writes=[ur], name=f"Z1_{k}_{q}")
        S.add("dve", t_dve(TT, psum=True),
              lambda v, ucur=ucur, bv=bv: v.tensor_tensor(out=ucur, in0=ucur, in1=bank(bv), op=ALU.mult),
              reads=[ur, ("bank", bv)], writes=[ur], name=f"Z2_{k}_{q}")
        a1, a2 = t1[q % 2], t2[q % 2]
        S.add("act", t_act(TT) + 0.1,
              lambda a, ucur=ucur, a1=a1, q=q: a.activation(out=a1[:, :], in_=ucur, func=AF.Copy,
                                                            scale=convw_col[:, 2, q:q + 1]),
              reads=[ur, "convw_col"], writes=[("t1", q % 2)], name=f"Z3_{k}_{q}")
        S.add("dve", t_dve(TT),
              lambda v, a1=a1, a2=a2, q=q: v.scalar_tensor_tensor(out=a2[:, :], in0=ubuf[:, q, 1:TT + 1],
                                                                  scalar=convw_col[:, 1, q:q + 1], in1=a1[:, :],
                                                                  op0=ALU.mult, op1=ALU.add),
              reads=[ur, ("t1", q % 2), "convw_col"], writes=[("t2", q % 2)], name=f"Z4_{k}_{q}")
        S.add("dve", t_dve(TT),
              lambda v, a1=a1, a2=a2, q=q: v.scalar_tensor_tensor(out=a1[:, :], in0=ubuf[:, q, 0:TT],
                                                                  scalar=convw_col[:, 0, q:q + 1], in1=a2[:, :],
                                                                  op0=ALU.mult, op1=ALU.add),
              reads=[ur, ("t2", q % 2), "convw_col"], writes=[("t1", q % 2)], name=f"Z5_{k}_{q}")

    def z_conv_post(k, q, bb):
        a1 = t1[q % 2]
        S.add("dve", t_dve(TT, psum=True),
              lambda v, a1=a1, q=q, bb=bb: v.tensor_tensor(out=ybf[:, q, :], in0=a1[:, :], in1=bank(bb), op=ALU.mult),
              reads=[("t1", q % 2), ("bank", bb)], writes=[("y", q)], name=f"Z6_{k}_{q}")

    def z_pool_ops(k, g, bp):
        pr = ("p", g)
        W = WINS[g]
        S.add("act", t_act(TT), lambda a, g=g, bp=bp: a.activation(out=psb[:, g, 16:TT + 16], in_=bank(bp), func=AF.Copy),
              reads=[("bank", bp)], writes=[pr], name=f"P1_{k}_{g}")
        A, B = wsA[0], wsB[0]
        ra, rb = ("wsA", 0), ("wsB", 0)
        src = psb[:, g, :]
        sres = pr
        cur, cres = None, None
        w = 1
        bufs = [(A, ra), (B, rb)]
        bi = 0
        while w < W:
            lo = 16 - (W - 2 * w)
            dstb, dres = bufs[bi]
            bi ^= 1
            if cur is None:
                i0, i1 = src[:, lo:TT + 16], src[:, lo - w:TT + 16 - w]
                rd = [sres]
            else:
                i0, i1 = cur[:, lo:TT + 16], cur[:, lo - w:TT + 16 - w]
                rd = [cres]
            S.add("dve", t_dve(TT + 16 - lo),
                  lambda v, o=dstb[:, lo:TT + 16], i0=i0, i1=i1: v.tensor_tensor(out=o, in0=i0, in1=i1, op=ALU.add),
                  reads=rd, writes=[dres], name=f"P2_{k}_{g}_{w}")
            cur, cres = dstb, dres
            w *= 2
        S.add("dve", t_dve(TT),
              lambda v, cur=cur, g=g, W=W: v.scalar_tensor_tensor(out=pooled[:, g, :], in0=cur[:, 16:TT + 16],
                                                                  scalar=1.0 / W, in1=psb[:, g, 16:TT + 16],
                                                                  op0=ALU.mult, op1=ALU.subtract),
              reads=[cres, pr], writes=[("y", 4 + g)], name=f"P3_{k}_{g}")

    def phase_IN(k, mid=None):
        halo_ops(k)
        banks = {}
        for n, j in enumerate(_DBG.get("in_order", IN_ORDER)):
            if n == 8 and mid is not None:
                mid()
            slot = meta_slots.pop(j) if (k == 0 and j in meta_slots) else ring_load("in", j)
            bk = in_n[0] % 4
            in_n[0] += 1
            banks[j] = bk
            S.add("pe", 8 * MM, pe_in(slot, bk, aT, TT), reads=[("ring", slot)] + [("aT", g) for g in range(4)],
                  writes=[("bank", bk)], name=f"IN_{k}_{j}")
            if j < 4:
                z_conv_post(k, j, bk)
            elif 8 <= j < 12:
                z_conv_pre(k, j - 8, banks[j - 4], bk)
            elif j >= 12:
                z_pool_ops(k, j - 12, bk)

    def phase_POOLMM(k):
        for g in range(4):
            bk = g
            S.add("pe", MM, lambda t, g=g, bk=bk: t.matmul(bank(bk), poolw_bf[:, g, :], pooled[:, g, :], start=True, stop=True),
                  reads=["poolw_bf", ("y", 4 + g)], writes=[("bank", bk)], name=f"PM_{k}_{g}")
            S.add("act", t_act(TT) + 0.1,
                  lambda a, g=g, bk=bk: a.activation(out=ybf[:, 4 + g, :], in_=bank(bk), func=AF.Copy,
                                                     scale=pscale_col[:, g:g + 1]),
                  reads=[("bank", bk), "pscale_col"], writes=[("y", 4 + g)], name=f"P4_{k}_{g}")

    def phase_OUT(k):
        slots = [ring_load("out", kc) for kc in range(8)]
        b = k % 3
        for tg in range(4):
            pi = tg % 2

            def f(t, tg=tg, pi=pi):
                ins = None
                for kc in range(8):
                    for nh in range(2):
                        ins = t.matmul(pp[pi][:, nh * 512:(nh + 1) * 512], ybf[:, kc, tg * 128:(tg + 1) * 128],
                                       ring[:, slots[kc], nh * 512:(nh + 1) * 512], start=(kc == 0), stop=(kc == 7))
                return ins
            bks = [("bank", 2 * pi), ("bank", 2 * pi + 1)]
            S.add("pe", 16 * MM, f, reads=[("ring", s) for s in slots] + [("y", c) for c in range(8)],
                  writes=bks, name=f"OUT_{k}_{tg}")
            xg = xbuf[b][:, tg, :]
            xr = ("xbuf", b, tg)
            S.add("act", t_act(1024),
                  lambda a, pi=pi, tg=tg: a.activation(out=tt[tg % 2][:, :], in_=pair(pi), func=AF.Square, scale=1.0 / 32.0,
                                                       accum_out=col(C_SSM, tg)),
                  reads=bks, writes=[("st", C_SSM + tg), ("tt", tg % 2)], name=f"M1_{k}_{tg}")
            rstd_ops(C_SSM, C_EM, C_RM, tg, f"m{k}_{tg}")
            tb = tt[tg % 2]
            S.add("dve", t_dve(1024, psum=True),
                  lambda v, pi=pi, tb=tb: v.tensor_tensor(out=tb[:, :], in0=pair(pi), in1=gpost_b[:, :], op=ALU.mult),
                  reads=bks + ["gpost_b", ("st", C_SSM + tg)], writes=[("tt", tg % 2)], name=f"M3_{k}_{tg}")
            S.add("dve", t_dve(1024),
                  lambda v, xg=xg, tb=tb, tg=tg: v.scalar_tensor_tensor(out=xg, in0=tb[:, :], scalar=col(C_RM, tg), in1=xg,
                                                                        op0=ALU.mult, op1=ALU.add),
                  reads=[("tt", tg % 2), ("st", C_RM + tg), xr], writes=[xr], name=f"M4_{k}_{tg}")
            S.add("act", t_act(1024),
                  lambda a, xg=xg, tg=tg: a.activation(out=fbf[tg][:, :], in_=xg, func=AF.Square, scale=1.0 / 32.0,
                                                       accum_out=col(C_SSH, tg)),
                  reads=[xr], writes=[("st", C_SSH + tg), ("fbf", tg)], name=f"M5_{k}_{tg}")
            rstd_ops(C_SSH, C_EH, C_RH, tg, f"h{k}_{tg}")
            fb = fbf[tg]
            S.add("act", t_act(1024) + 0.1,
                  lambda a, xg=xg, fb=fb, tg=tg: a.activation(out=fb[:, :], in_=xg, func=AF.Copy, scale=col(C_RH, tg)),
                  reads=[xr, ("st", C_RH + tg)], writes=[("fbf", tg)], name=f"M7_{k}_{tg}")

    def phase_Tf(k):
        for tg in range(4):
            bk = 4 + tg
            S.add("pe", 8 * TP, pe_transposes(fbf[tg], bk), reads=[("fbf", tg), "ident"],
                  writes=[("bank", bk)], name=f"Tf_{k}_{tg}")
            S.add("act", t_act(1024),
                  lambda a, bk=bk, tg=tg: a.activation(out=fT[:, :, tg * 128:(tg + 1) * 128], in_=bank_bf(bk),
                                                       func=AF.Copy),
                  reads=[("bank", bk)], writes=[("fT", tg)], name=f"M8_{k}_{tg}")

    def phase_GU(k, mid=None, late=None):
        for j in range(NFF):
            if j == NFF // 2 and mid is not None:
                mid()
            if j == 16 and late is not None:
                late()
            bg, bu = 4 + 2 * (j % 2), 5 + 2 * (j % 2)
            for which, bk in ((0, bg), (1, bu)):
                slot = ring_load("gu", 2 * j + which)
                S.add("pe", 8 * MM, pe_in(slot, bk, fT, TT), reads=[("ring", slot)] + [("fT", g) for g in range(4)],
                      writes=[("bank", bk)], name=f"GU_{k}_{j}_{which}")
            sl = sil[j % 2]
            S.add("act", t_act(TT), lambda a, sl=sl, bg=bg: a.activation(out=sl[:, :], in_=bank(bg), func=AF.Silu),
                  reads=[("bank", bg)], writes=[("sil", j % 2)], name=f"G1_{k}_{j}")
            S.add("dve", t_dve(TT, psum=True),
                  lambda v, sl=sl, bu=bu, j=j: v.tensor_tensor(out=gbf[:, j, :], in0=sl[:, :], in1=bank(bu), op=ALU.mult),
                  reads=[("sil", j % 2), ("bank", bu)], writes=[("g", j)], name=f"G2_{k}_{j}")

    DN_PAIR = {0: 2, 1: 3, 2: 0, 3: 1}
    JB = 4

    def phase_DOWN(k):
        b = k % 3
        blocks = [list(range(0, 8)), list(range(8, 12)), list(range(12, 16)), list(range(16, NFF))]

        def part(j, slot, tgs, tag):
            def f(t, j=j, slot=slot, tgs=tgs):
                ins = None
                for tg in tgs:
                    for nh in range(2):
                        ins = t.matmul(pp[DN_PAIR[tg]][:, nh * 512:(nh + 1) * 512], gbf[:, j, tg * 128:(tg + 1) * 128],
                                       ring[:, slot, nh * 512:(nh + 1) * 512], start=(j == 0), stop=(j == NFF - 1))
                return ins
            bks = []
            for tg in tgs:
                bks += [("bank", 2 * DN_PAIR[tg]), ("bank", 2 * DN_PAIR[tg] + 1)]
            S.add("pe", 4 * MM, f, reads=[("ring", slot), ("g", j)], writes=bks, name=f"DN_{k}_{tag}_{j}")

        for bi, blk in enumerate(blocks):
            slots = {j: ring_load("dn", j) for j in blk}
            order_ = (("B", (0, 1)), ("A", (2, 3)))
            if bi == len(blocks) - 1:
                order_ = order_[::-1]
            for tag, tgs in order_:
                for j in blk:
                    part(j, slots[j], tgs, tag)
        for tgs in ((2, 3), (0, 1)):
            for tg in tgs:
                pi = DN_PAIR[tg]
                pb = [("bank", 2 * pi), ("bank", 2 * pi + 1)]
                xg = xbuf[b][:, tg, :]
                xr = ("xbuf", b, tg)
                S.add("act", t_act(1024),
                      lambda a, pi=pi, tg=tg: a.activation(out=tt[tg % 2][:, :], in_=pair(pi), func=AF.Square, scale=1.0 / 32.0,
                                                           accum_out=col(C_SSD, tg)),
                      reads=pb, writes=[("st", C_SSD + tg), ("tt", tg % 2)], name=f"D1_{k}_{tg}")
                rstd_ops(C_SSD, C_ED, C_RD, tg, f"d{k}_{tg}")
                tb = tt[tg % 2]
                S.add("dve", t_dve(1024, psum=True),
                      lambda v, pi=pi, tb=tb: v.tensor_tensor(out=tb[:, :], in0=pair(pi), in1=gfpost_b[:, :], op=ALU.mult),
                      reads=pb + ["gfpost_b", ("st", C_SSD + tg)], writes=[("tt", tg % 2)], name=f"D3_{k}_{tg}")
                S.add("dve", t_dve(1024),
                      lambda v, xg=xg, tb=tb, tg=tg: v.scalar_tensor_tensor(out=xg, in0=tb[:, :], scalar=col(C_RD, tg), in1=xg,
                                                                            op0=ALU.mult, op1=ALU.add),
                      reads=[("tt", tg % 2), ("st", C_RD + tg), xr], writes=[xr], name=f"D4_{k}_{tg}")
            store_out(k, list(tgs))

    def phase_meta_load():
        mb = tt[0]
        key = ("meta",)
        dsem(key)

        def ldm(sp):
            return sp.dma_start(out=mb[0:NMETA, :], in_=meta_d[:, :]).then_inc(dma_sem[key], 16)
        S.add("sp", 0.1, ldm, writes=[("tt", 0)], dma=(key, 64e3), name="ld_meta")

    def phase_meta():
        mb = tt[0]
        S.add("act", t_act(1024),
              lambda a: a.activation(out=abf[0][0:NMETA, :], in_=mb[0:NMETA, :], func=AF.Square, scale=1.0 / 32.0,
                                     accum_out=st[0:NMETA, C_SSX:C_SSX + 1]),
              reads=[("tt", 0)], writes=[("st", C_SSX), ("abf", 0)], name="X1_meta")
        S.add("pool", 0.15, lambda g: g.tensor_scalar(out=st[0:NMETA, C_EX:C_EX + 1], in0=st[0:NMETA, C_SSX:C_SSX + 1],
                                                      scalar1=EPS, scalar2=None, op0=ALU.add),
              reads=[("st", C_SSX)], writes=[("st", C_EX)], name="eps_meta")
        S.add("pool", 0.15, lambda g: g.tensor_tensor(out=st[0:NMETA, C_RX:C_RX + 1], in0=st[0:NMETA, C_EX:C_EX + 1],
                                                      in1=nhalf[0:NMETA, :], op=ALU.pow),
              reads=[("st", C_EX), "nhalf"], writes=[("st", C_RX)], name="pow_meta")
        S.add("act", 1.2, lambda a: a.activation(out=abf[0][0:NMETA, :], in_=mb[0:NMETA, :], func=AF.Copy,
                                                 scale=st[0:NMETA, C_RX:C_RX + 1]),
              reads=[("tt", 0), ("st", C_RX)], writes=[("abf", 0)], name="X3_meta")
        S.add("pe", 8 * TP, pe_transposes(abf[0], 0, rows=NMETA), reads=[("abf", 0), "ident"], writes=[("bank", 0)],
              name="Ta_meta")
        S.add("act", 0.4, lambda a: a.activation(out=aTm[:, :, :], in_=bank_bf(0)[:, :, 0:NMETA], func=AF.Copy),
              reads=[("bank", 0)], writes=["aTm"], name="X4_meta")
        banks = {}
        for n, j in enumerate(_DBG.get("in_order", IN_ORDER)):
            if j < 4:
                continue
            slot = ring_load("in", j)
            meta_slots[j] = slot
            bk = in_n[0] % 4
            in_n[0] += 1
            banks[j] = bk
            S.add("pe", 8 * 0.05, pe_in(slot, bk, aTm, NMETA), reads=[("ring", slot), "aTm"], writes=[("bank", bk)],
                  name=f"INm_{j}")
            if 8 <= j < 12:
                q = j - 8
                bc = banks[4 + q]
                S.add("act", 0.3, lambda a, q=q, bc=bc: a.activation(out=umeta[:, q, :], in_=bank(bc)[:, 0:NMETA], func=AF.Copy),
                      reads=[("bank", bc)], writes=[("um", q)], name=f"Z1m_{q}")
                S.add("dve", 0.2, lambda v, q=q, bk=bk: v.tensor_tensor(out=umeta[:, q, :], in0=umeta[:, q, :],
                                                                        in1=bank(bk)[:, 0:NMETA], op=ALU.mult),
                      reads=[("um", q), ("bank", bk)], writes=[("um", q), "umeta"], name=f"Z2m_{q}")
            elif j >= 12:
                g = j - 12
                S.add("act", 0.3, lambda a, g=g, bk=bk: a.activation(out=pmeta[:, g, :], in_=bank(bk)[:, 0:NMETA], func=AF.Copy),
                      reads=[("bank", bk)], writes=[("pm", g), "pmeta"], name=f"P1m_{g}")

    const_setup()
    phase_meta()
    const_setup2()
    for k in range(min(3, ntile)):
        load_x(k)
    def x_and_ta(k):
        if k < ntile:
            for tg in range(4):
                phase_X(k, tg)
                phase_Ta(k, tg)

    x_and_ta(0)
    phase_IN(0)
    const_setup3()
    phase_POOLMM(0)
    x_and_ta(1)
    phase_OUT(0)
    for k in range(ntile):
        if k + 1 < ntile:
            phase_IN(k + 1, mid=lambda k=k: phase_Tf(k))
            phase_GU(k, mid=lambda k=k: phase_POOLMM(k + 1), late=lambda k=k: x_and_ta(k + 2))
            phase_OUT(k + 1)
        else:
            phase_Tf(k)
            phase_GU(k)
        phase_DOWN(k)
        if k + 3 < ntile:
            load_x(k + 3)

    if debug:
        dbg_d = nc.dram_tensor("dbg", [128, 64], F32, kind="ExternalOutput").ap()
        dsem(("dbg",))

        def dbgf(sp):
            return sp.dma_start(out=dbg_d[:, :], in_=st[:, :]).then_inc(dma_sem[("dbg",)], 16)
        S.add("sp", 0.1, dbgf, reads=[("st", c) for c in range(48)] + [("outdone", ntile - 1, 0)],
              writes=[("outdone", "dbg", 0)], dma=(("dbg",), 32e3), name="st_dbg")
    order = S.simulate()
    if verbose:
        print(f"[sched] ops={len(S.ops)} sim_time={S.sim_time:.1f}us "
              + " ".join(f"{e}={len(order[e])}" for e in order))

    ops = S.ops
    for e in ("pe", "act", "dve", "pool"):
        for n, i in enumerate(order[e]):
            ops[i].sig = (eng_sem[e], n + 1)
    cnt = {}
    for i in order["sp"]:
        op = ops[i]
        key = op.dma[0]
        cnt[key] = cnt.get(key, 0) + 16
        op.sig = (dma_sem[key], cnt[key])

    finals = [i for i, op in enumerate(ops) if op.eng == "sp" and op.name.startswith("st")]

    def emit_engine(e, eng):
        seen = {}
        for i in order[e]:
            op = ops[i]
            for d in op.deps:
                dop = ops[d]
                if e == "pe" and dop.eng == "pe":
                    continue
                sem, val = dop.sig
                key = id(sem)
                if seen.get(key, 0) >= val:
                    continue
                seen[key] = val
                eng.wait_ge(sem, val)
            ins = op.emit(eng)
            if op.dma is None:
                ins.then_inc(eng_sem[e], 1)
        if e == "sp":
            for i in finals:
                sem, val = ops[i].sig
                eng.wait_ge(sem, val)

    block = es.enter_context(nc.Block())

    @block.sync
    def _(sp):
        emit_engine("sp", sp)

    @block.tensor
    def _(t):
        emit_engine("pe", t)

    @block.scalar
    def _(a):
        emit_engine("act", a)

    @block.vector
    def _(v):
        emit_engine("dve", v)

    @block.gpsimd
    def _(g):
        emit_engine("pool", g)

    es.close()
    return nc, S


_CACHE = {}


def _get_program(nseq, nt):
    key = (nseq, nt)
    if key not in _CACHE:
        _CACHE[key] = build_program(nseq, nt)[0]
    return _CACHE[key]


def kernel(x, meta_tokens, norm_mix_pre, w_in, conv_w, pool_w, pool_scale, w_out, norm_mix_post,
           norm_ffn_pre, w_gate, w_up, w_down, norm_ffn_post):
    x = np.asarray(x)
    B, SEQ, _ = x.shape
    ncores = 8
    nseq = B // ncores
    nt = SEQ // TT
    nc = _get_program(nseq, nt)
    f = lambda a: np.ascontiguousarray(np.asarray(a, dtype=np.float32))
    shared = {
        "meta_tokens": f(meta_tokens),
        "norm_mix_pre": f(norm_mix_pre).reshape(D),
        "w_in": f(w_in).reshape(D, 2 * D),
        "conv_w": f(conv_w).reshape(3, 512),
        "pool_w": f(pool_w).reshape(4, 128, 128),
        "pool_scale": f(pool_scale).reshape(512),
        "w_out": f(w_out).reshape(D, D),
        "norm_mix_post": f(norm_mix_post).reshape(D),
        "norm_ffn_pre": f(norm_ffn_pre).reshape(D),
        "w_gate": f(w_gate).reshape(D, DFF),
        "w_up": f(w_up).reshape(D, DFF),
        "w_down": f(w_down).reshape(DFF, D),
        "norm_ffn_post": f(norm_ffn_post).reshape(D),
    }
    xs = f(x).reshape(ncores, nseq * SEQ, D)
    in_maps = [dict(shared, x=xs[c]) for c in range(ncores)]
    res = run_bass_kernel_spmd(nc, in_maps, core_ids=list(range(ncores)))
    out = np.stack([np.asarray(r["out"]) for r in res.results], axis=0)
    return out.reshape(B, SEQ, D).astype(np.float32, copy=False)
```
